# Optimizing a Trainium2 kernel written in Bass

```python
import math
import jax, jax.numpy as jnp
from jax import lax
import numpy as np

D_MODEL = 2048
BATCH = 2
SEQ = 4096
DEPTH = 4

A_HEAD_DIM = 128
A_WIDTH = D_MODEL // 2
A_HEADS = A_WIDTH // A_HEAD_DIM
DILATED_BRANCHES = ((128, 1), (512, 4), (2048, 16))
BAND_BLOCK = 128
B_CHANNELS = D_MODEL - A_WIDTH
CONV_WIDTH = 31
C_HEADS = 4
C_V_WIDTH = D_MODEL
C_V_DIM = C_V_WIDTH // C_HEADS
C_QK_DIM = C_V_DIM // 2
C_QK_WIDTH = C_HEADS * C_QK_DIM
C_CHUNK = 64
FFN_HIDDEN = 4 * D_MODEL
DEEPNORM_ALPHA = (2 * DEPTH) ** 0.25
DEEPNORM_BETA = (8 * DEPTH) ** -0.25
LN_EPS = 1e-5
N_EVEN = (DEPTH + 1) // 2
N_ODD = DEPTH // 2
EVEN_IN_WIDTH = 3 * A_WIDTH + 2 * B_CHANNELS
ODD_IN_WIDTH = 2 * C_QK_WIDTH + 2 * C_V_WIDTH + 2 * C_HEADS

kernel_name = "dilated_conv_mlstm_deepnorm_hybrid"


def layer_norm(x, g, b):
    xf = x.astype(jnp.float32)
    mu = jnp.mean(xf, axis=-1, keepdims=True)
    var = jnp.mean(jnp.square(xf - mu), axis=-1, keepdims=True)
    return ((xf - mu) * lax.rsqrt(var + LN_EPS) * g + b).astype(x.dtype)


def alibi_slopes(n):
    return 2.0 ** (-8.0 * jnp.arange(1, n + 1, dtype=jnp.float32) / n)


def dilated_branch(q, k, v, slopes, window, dil):
    bsz, seq, nh, hd = q.shape
    span = window // dil
    P = BAND_BLOCK
    ls = seq // dil
    nb = -(-ls // P)
    lp = nb * P

    def to_sub(t):
        t = t.reshape(bsz, ls, dil, nh, hd).transpose(0, 2, 1, 3, 4)
        t = jnp.pad(t, ((0, 0), (0, 0), (0, lp - ls), (0, 0), (0, 0)))
        return t.reshape(bsz, dil, nb, P, nh, hd)

    def with_prev(t):
        prev = jnp.pad(t, ((0, 0), (0, 0), (1, 0), (0, 0), (0, 0), (0, 0)))[:, :, :-1]
        return jnp.concatenate([prev, t], axis=3)

    qb = to_sub(q)
    kk = with_prev(to_sub(k))
    vv = with_prev(to_sub(v))
    s = jnp.einsum('brnqhe,brnkhe->brnhqk', qb, kk, preferred_element_type=jnp.float32)
    qi = jnp.arange(P)[:, None]
    ki = jnp.arange(2 * P)[None, :]
    dist = P + qi - ki
    blk = jnp.arange(nb)[:, None, None]
    valid = (dist >= 0) & (dist <= span) & (blk * P - P + ki >= 0)
    s = s - slopes[:, None, None] * (dist * dil).astype(jnp.float32)
    s = jnp.where(valid[:, None], s, -jnp.inf)
    lse = jax.nn.logsumexp(s, axis=-1)
    p = jnp.exp(s - lse[..., None])
    o = jnp.einsum('brnhqk,brnkhe->brnqhe', p, vv.astype(jnp.float32))

    def from_sub(t):
        t = t.reshape((bsz, dil, lp) + t.shape[4:])[:, :, :ls]
        t = jnp.moveaxis(t, 1, 2)
        return t.reshape((bsz, seq) + t.shape[3:])

    return from_sub(o), from_sub(lse.transpose(0, 1, 2, 4, 3))


def dilated_mixture_attention(q, k, v):
    slopes = alibi_slopes(A_HEADS)
    outs, lses = [], []
    for window, dil in DILATED_BRANCHES:
        o, l = dilated_branch(q, k, v, slopes, window, dil)
        outs.append(o)
        lses.append(l)
    wts = jax.nn.softmax(jnp.stack(lses), axis=0)
    return jnp.sum(wts[..., None] * jnp.stack(outs), axis=0)


def conformer_conv(u, conv_w, conv_b, ln_g, ln_b):
    a, g = jnp.split(u, 2, axis=-1)
    h = a * jax.nn.sigmoid(g)
    h = lax.conv_general_dilated(h, conv_w[:, None, :], window_strides=(1,),
                                 padding=((CONV_WIDTH - 1, 0),),
                                 dimension_numbers=('NWC', 'WIO', 'NWC'),
                                 feature_group_count=B_CHANNELS) + conv_b
    return jax.nn.silu(layer_norm(h, ln_g, ln_b))


def mlstm_chunkwise(q, k, v, ig, lf):
    bsz, nh, seq, dk = q.shape
    dv = v.shape[-1]
    nc = seq // C_CHUNK

    def chunks(t):
        return jnp.moveaxis(t.reshape((bsz, nh, nc, C_CHUNK) + t.shape[3:]), 2, 0)

    causal = jnp.tril(jnp.ones((C_CHUNK, C_CHUNK), dtype=bool))

    def step(carry, inp):
        c_st, n_st, m_st = carry
        qc, kc, vc, ic, fc = inp
        b = jnp.cumsum(fc, axis=-1)
        dmat = jnp.where(causal, b[..., :, None] - b[..., None, :] + ic[..., None, :], -jnp.inf)
        inter = b + m_st[..., None]
        m_t = jnp.maximum(inter, jnp.max(dmat, axis=-1))
        w = jnp.exp(dmat - m_t[..., None])
        a = jnp.exp(inter - m_t)
        sqk = jnp.einsum('bhtd,bhsd->bhts', qc, kc) * w
        num = a[..., None] * jnp.einsum('bhvd,bhtd->bhtv', c_st, qc) + jnp.einsum('bhts,bhsv->bhtv', sqk, vc)
        den = a * jnp.einsum('bhd,bhtd->bht', n_st, qc) + jnp.sum(sqk, axis=-1)
        h = num / jnp.maximum(jnp.abs(den), jnp.exp(-m_t))[..., None]
        b_last = b[..., -1]
        g = b_last[..., None] - b + ic
        m_new = jnp.maximum(b_last + m_st, jnp.max(g, axis=-1))
        decay = jnp.exp(b_last + m_st - m_new)
        ws = jnp.exp(g - m_new[..., None])
        c_new = decay[..., None, None] * c_st + jnp.einsum('bhs,bhsv,bhsd->bhvd', ws, vc, kc)
        n_new = decay[..., None] * n_st + jnp.einsum('bhs,bhsd->bhd', ws, kc)
        return (c_new, n_new, m_new), h

    init = (jnp.zeros((bsz, nh, dv, dk), jnp.float32),
            jnp.zeros((bsz, nh, dk), jnp.float32),
            jnp.zeros((bsz, nh), jnp.float32))
    _, hs = lax.scan(step, init, (chunks(q), chunks(k), chunks(v), chunks(ig), chunks(lf)))
    return jnp.moveaxis(hs, 0, 2).reshape(bsz, nh, seq, dv).transpose(0, 2, 1, 3)


def even_mixer(x, w_in, conv_w, conv_b, conv_ln_g, conv_ln_b, w_out):
    bsz, seq, _ = x.shape
    proj = x @ w_in
    q, k, v, u = jnp.split(proj, [A_WIDTH, 2 * A_WIDTH, 3 * A_WIDTH], axis=-1)
    q = q.reshape(bsz, seq, A_HEADS, A_HEAD_DIM) * (A_HEAD_DIM ** -0.5)
    k = k.reshape(bsz, seq, A_HEADS, A_HEAD_DIM)
    v = v.reshape(bsz, seq, A_HEADS, A_HEAD_DIM)
    att = dilated_mixture_attention(q, k, v).reshape(bsz, seq, A_WIDTH).astype(x.dtype)
    conv = conformer_conv(u, conv_w, conv_b, conv_ln_g, conv_ln_b)
    return jnp.concatenate([att, conv], axis=-1) @ w_out


def odd_mixer(x, w_in, igate_b, fgate_b, norm_g, w_out):
    bsz, seq, _ = x.shape
    proj = x @ w_in
    q, k, v, o, ig, fg = jnp.split(proj, [C_QK_WIDTH, 2 * C_QK_WIDTH, 2 * C_QK_WIDTH + C_V_WIDTH,
                                          2 * C_QK_WIDTH + 2 * C_V_WIDTH,
                                          2 * C_QK_WIDTH + 2 * C_V_WIDTH + C_HEADS], axis=-1)

    def to_heads(t, d):
        return t.reshape(bsz, seq, C_HEADS, d).transpose(0, 2, 1, 3).astype(jnp.float32)

    igf = (ig + igate_b).astype(jnp.float32).transpose(0, 2, 1)
    lf = jax.nn.log_sigmoid((fg + fgate_b).astype(jnp.float32)).transpose(0, 2, 1)
    h = mlstm_chunkwise(to_heads(q, C_QK_DIM) * (C_QK_DIM ** -0.5), to_heads(k, C_QK_DIM),
                        to_heads(v, C_V_DIM), igf, lf)
    mu = jnp.mean(h, axis=-1, keepdims=True)
    var = jnp.mean(jnp.square(h - mu), axis=-1, keepdims=True)
    hn = ((h - mu) * lax.rsqrt(var + LN_EPS)).reshape(bsz, seq, C_V_WIDTH) * norm_g
    y = (jax.nn.sigmoid(o.astype(jnp.float32)) * hn).astype(x.dtype)
    return y @ w_out


def setup_inputs(seed: int = 0) -> dict:
    key = jax.random.key(seed)
    ks = iter(jax.random.split(key, 32))

    def nrm(shape, scale):
        return scale * jax.random.normal(next(ks), shape, jnp.float32)

    D = D_MODEL
    x = nrm((BATCH, SEQ, D), 1.0)
    even_col = jnp.ones((EVEN_IN_WIDTH,), jnp.float32).at[2 * A_WIDTH:3 * A_WIDTH].set(DEEPNORM_BETA)
    even_w_in = nrm((N_EVEN, D, EVEN_IN_WIDTH), D ** -0.5) * even_col
    even_conv_w = nrm((N_EVEN, CONV_WIDTH, B_CHANNELS), CONV_WIDTH ** -0.5)
    even_conv_b = nrm((N_EVEN, B_CHANNELS), 0.02)
    even_conv_ln_g = 1.0 + nrm((N_EVEN, B_CHANNELS), 0.05)
    even_conv_ln_b = nrm((N_EVEN, B_CHANNELS), 0.02)
    even_w_out = nrm((N_EVEN, D, D), D ** -0.5 * DEEPNORM_BETA)
    odd_col = jnp.ones((ODD_IN_WIDTH,), jnp.float32).at[2 * C_QK_WIDTH:2 * C_QK_WIDTH + C_V_WIDTH].set(DEEPNORM_BETA)
    odd_w_in = nrm((N_ODD, D, ODD_IN_WIDTH), D ** -0.5) * odd_col
    odd_igate_b = nrm((N_ODD, C_HEADS), 0.1)
    odd_fgate_b = jnp.linspace(3.0, 6.0, C_HEADS, dtype=jnp.float32)[None, :] + nrm((N_ODD, C_HEADS), 0.1)
    odd_norm_g = 1.0 + nrm((N_ODD, C_V_WIDTH), 0.05)
    odd_w_out = nrm((N_ODD, C_V_WIDTH, D), C_V_WIDTH ** -0.5 * DEEPNORM_BETA)
    mix_ln_g = 1.0 + nrm((DEPTH, D), 0.05)
    mix_ln_b = nrm((DEPTH, D), 0.02)
    ffn_w1 = nrm((DEPTH, D, FFN_HIDDEN), D ** -0.5 * DEEPNORM_BETA)
    ffn_w2 = nrm((DEPTH, FFN_HIDDEN, D), FFN_HIDDEN ** -0.5 * DEEPNORM_BETA)
    ffn_ln_g = 1.0 + nrm((DEPTH, D), 0.05)
    ffn_ln_b = nrm((DEPTH, D), 0.02)
    return {"x": x, "even_w_in": even_w_in, "even_conv_w": even_conv_w, "even_conv_b": even_conv_b,
            "even_conv_ln_g": even_conv_ln_g, "even_conv_ln_b": even_conv_ln_b, "even_w_out": even_w_out,
            "odd_w_in": odd_w_in, "odd_igate_b": odd_igate_b, "odd_fgate_b": odd_fgate_b,
            "odd_norm_g": odd_norm_g, "odd_w_out": odd_w_out, "mix_ln_g": mix_ln_g, "mix_ln_b": mix_ln_b,
            "ffn_w1": ffn_w1, "ffn_w2": ffn_w2, "ffn_ln_g": ffn_ln_g, "ffn_ln_b": ffn_ln_b}


def reference(x, even_w_in, even_conv_w, even_conv_b, even_conv_ln_g, even_conv_ln_b, even_w_out,
              odd_w_in, odd_igate_b, odd_fgate_b, odd_norm_g, odd_w_out, mix_ln_g, mix_ln_b,
              ffn_w1, ffn_w2, ffn_ln_g, ffn_ln_b):
    for layer in range(DEPTH):
        j = layer // 2
        if layer % 2 == 0:
            mix = even_mixer(x, even_w_in[j], even_conv_w[j], even_conv_b[j],
                             even_conv_ln_g[j], even_conv_ln_b[j], even_w_out[j])
        else:
            mix = odd_mixer(x, odd_w_in[j], odd_igate_b[j], odd_fgate_b[j], odd_norm_g[j], odd_w_out[j])
        x = layer_norm(DEEPNORM_ALPHA * x + mix, mix_ln_g[layer], mix_ln_b[layer])
        hid = jnp.square(jax.nn.relu(x @ ffn_w1[layer]))
        x = layer_norm(DEEPNORM_ALPHA * x + hid @ ffn_w2[layer], ffn_ln_g[layer], ffn_ln_b[layer])
    return x
```

```python
from contextlib import ExitStack
import numpy as np
import ml_dtypes
import concourse.bass as bass
import concourse.mybir as mybir
from concourse.bass_utils import run_bass_kernel_spmd

F32 = mybir.dt.float32
BF16 = mybir.dt.bfloat16
AF = mybir.ActivationFunctionType
ALU = mybir.AluOpType
AX = mybir.AxisListType
NPBF = ml_dtypes.bfloat16


class Buf:
    __slots__ = ("name", "last_w", "readers")

    def __init__(self, name=""):
        self.name = name
        self.last_w = None
        self.readers = []


class Prog:
    ENG = ("pe", "act", "dve", "pool", "sp")
    BLK = {"pe": "tensor", "act": "scalar", "dve": "vector", "pool": "gpsimd", "sp": "sync"}
    NPOOL = {"sp": 24, "pool": 8, "act": 8}

    def __init__(self):
        self.nc = bass.Bass("TRN2", target_bir_lowering=False)
        self.es = ExitStack()
        self.ins = []
        self.nbuf = 0

    def sb(self, name, shape, dt):
        return self.es.enter_context(self.nc.sbuf_tensor(name, list(shape), dt))

    def ps(self, name, shape, dt=F32):
        return self.es.enter_context(self.nc.psum_tensor(name, list(shape), dt))

    def din(self, name, shape, dt):
        return self.nc.dram_tensor(name, list(shape), dt, kind="ExternalInput").ap()

    def dout(self, name, shape, dt):
        return self.nc.dram_tensor(name, list(shape), dt, kind="ExternalOutput").ap()

    def buf(self, name=""):
        self.nbuf += 1
        return Buf(name or f"b{self.nbuf}")

    def bufs(self, n, name=""):
        return [self.buf(f"{name}{i}") for i in range(n)]

    def add(self, eng, fn, reads=(), writes=(), dma=False):
        i = len(self.ins)
        deps = set()
        for b in reads:
            if b.last_w is not None:
                deps.add(b.last_w)
        for b in writes:
            if b.last_w is not None:
                deps.add(b.last_w)
            deps.update(b.readers)
        for b in reads:
            b.readers.append(i)
        for b in writes:
            b.last_w = i
            b.readers = []
        deps.discard(i)
        best = {}
        keep = set()
        for d in deps:
            de, _, _, ddma = self.ins[d]
            if ddma:
                keep.add(d)
            elif best.get(de, -1) < d:
                best[de] = d
        keep.update(best.values())
        deps = keep
        self.ins.append((eng, fn, deps, dma))
        return i

    def dma(self, q, out, in_, reads=(), writes=()):
        return self.add(q, lambda e: e.dma_start(out=out, in_=in_), reads, writes, dma=True)

    def mm(self, out, lhsT, rhs, start, stop, reads=(), writes=()):
        return self.add("pe", lambda e: e.matmul(out, lhsT, rhs, start=start, stop=stop), reads, writes)

    def final_wait(self, bufs):
        self.add("sp", lambda e: None, reads=bufs)

    def _skip(self, eng, d):
        de, _, _, ddma = self.ins[d]
        return eng == "pe" and de == "pe" and not ddma

    def emit(self):
        nc, es = self.nc, self.es
        needed = set()
        for (eng, fn, deps, dma) in self.ins:
            for d in deps:
                if not self._skip(eng, d):
                    needed.add(d)
        EPOCH = 2000
        semc = {e: [] for e in ("pe", "act", "dve", "pool")}
        dpool = {q: [es.enter_context(nc.semaphore(f"sd_{q}{k}")) for k in range(n)]
                 for q, n in self.NPOOL.items()}
        cnt = {e: 0 for e in semc}
        dcnt = {q: 0 for q in dpool}
        token = {}
        for i, (eng, fn, deps, dma) in enumerate(self.ins):
            if dma:
                j = dcnt[eng]
                dcnt[eng] += 1
                n = self.NPOOL[eng]
                token[i] = (dpool[eng][j % n], 16 * (j // n + 1))
            elif i in needed:
                ep, v = divmod(cnt[eng], EPOCH)
                if ep >= len(semc[eng]):
                    semc[eng].append(es.enter_context(nc.semaphore(f"sc_{eng}{ep}")))
                cnt[eng] += 1
                token[i] = (semc[eng][ep], v + 1)
        self.stats = dict(cnt=dict(cnt), dcnt=dict(dcnt), n=len(self.ins))
        with nc.Block() as block:
            for e in self.ENG:
                def body(engobj, e=e):
                    known = {}
                    for i, (eng, fn, deps, dma) in enumerate(self.ins):
                        if eng != e:
                            continue
                        for d in sorted(deps):
                            if self._skip(eng, d):
                                continue
                            sem, val = token[d]
                            if known.get(id(sem), 0) >= val:
                                continue
                            engobj.wait_ge(sem, val)
                            known[id(sem)] = val
                        r = fn(engobj)
                        if i in token:
                            r.then_inc(token[i][0], 16 if dma else 1)
                getattr(block, self.BLK[e])(body)
        self.es.close()
        return nc


def run(nc, in_maps, trace=False):
    res = run_bass_kernel_spmd(nc, in_maps, core_ids=list(range(len(in_maps))), trace=trace)
    return res


D = 2048
KC = 16
HID = 8192
HC = 64
TG = 512
ALPHA = 8 ** 0.25
EPS = 1e-5


def layer_norm_fm(P, xs, xs_b, mb, mb_b, gcol, bcol, S, nk=KC, final=None, ones=None):
    pst, psb = S["pst"], S["psb"]
    sq, sq_b, tmp, tmp_b = S["sq"], S["sq_b"], S["tmp"], S["tmp_b"]
    onesN, ones_b = (ones or S["onesN"]), S["ones_b"]
    mean, rstd, nmr, st_b = S["mean"], S["rstd"], S["nmr"], S["st_b"]
    pm, pe2 = pst[4], pst[5]
    for kc in range(nk):
        P.mm(pm[:, :], onesN[:, :], xs[:, kc, :], kc == 0, kc == nk - 1, reads=[ones_b, xs_b[kc]], writes=[psb[4]])
    for kc in range(nk):
        j = kc % 2
        P.add("act", lambda e, kc=kc, j=j: e.activation(out=sq[j][:, :], in_=xs[:, kc, :], func=AF.Square),
              reads=[xs_b[kc]], writes=[sq_b[j]])
        P.mm(pe2[:, :], onesN[:, :], sq[j][:, :], kc == 0, kc == nk - 1, reads=[ones_b, sq_b[j]], writes=[psb[5]])
    P.add("act", lambda e: e.activation(out=mean[:, :], in_=pm[:, :], func=AF.Copy), reads=[psb[4]], writes=[st_b[0]])
    P.add("act", lambda e: e.activation(out=nmr[:, :], in_=pm[:, :], func=AF.Square), reads=[psb[4]], writes=[st_b[2]])
    P.add("dve", lambda e: e.tensor_tensor(out=rstd[:, :], in0=pe2[:, :], in1=nmr[:, :], op=ALU.subtract),
          reads=[psb[5], st_b[2]], writes=[st_b[1]])
    P.add("dve", lambda e: e.tensor_scalar(out=rstd[:, :], in0=rstd[:, :], scalar1=EPS, scalar2=None, op0=ALU.add),
          reads=[st_b[1]], writes=[st_b[1]])
    P.add("act", lambda e: e.activation(out=rstd[:, :], in_=rstd[:, :], func=AF.Sqrt), reads=[st_b[1]], writes=[st_b[1]])
    P.add("dve", lambda e: e.reciprocal(out=rstd[:, :], in_=rstd[:, :]), reads=[st_b[1]], writes=[st_b[1]])
    P.add("dve", lambda e: e.scalar_tensor_tensor(out=nmr[:, :], in0=mean[:, :], scalar=-1.0, in1=rstd[:, :],
                                                  op0=ALU.mult, op1=ALU.mult),
          reads=[st_b[0], st_b[1]], writes=[st_b[2]])
    for kc in range(nk):
        j = kc % 2
        P.add("dve", lambda e, kc=kc, j=j: e.tensor_tensor(out=tmp[j][:, :], in0=xs[:, kc, :], in1=rstd[:, :], op=ALU.mult),
              reads=[xs_b[kc], st_b[1]], writes=[tmp_b[j]])
        P.add("dve", lambda e, j=j: e.tensor_tensor(out=tmp[j][:, :], in0=tmp[j][:, :], in1=nmr[:, :], op=ALU.add),
              reads=[tmp_b[j], st_b[2]], writes=[tmp_b[j]])
        if final is not None:
            final(kc, tmp[j], tmp_b[j])
            continue
        P.add("act", lambda e, kc=kc, j=j: e.activation(out=xs[:, kc, :], in_=tmp[j][:, :], func=AF.Identity,
                                                       scale=gcol[:, kc:kc + 1], bias=bcol[:, kc:kc + 1]),
              reads=[tmp_b[j], S["lnp_b"]], writes=[xs_b[kc]])
        P.add("pool", lambda e, kc=kc: e.tensor_copy(out=mb[:, kc, :], in_=xs[:, kc, :]),
              reads=[xs_b[kc]], writes=[mb_b[kc]])


def build_tail(ntok=1024):
    P = Prog()
    ntg = ntok // TG
    mT = P.din("mT", [D, ntok], BF16)
    xT = P.din("xT", [D, ntok], F32)
    wout = P.din("wout", [D, D], BF16)
    w1 = P.din("w1", [D, HID], BF16)
    w2 = P.din("w2", [HID, D], BF16)
    lnp = P.din("lnp", [128, 4 * KC], F32)
    oT = P.dout("oT", [D, ntok], F32)
    oTb = P.dout("oTb", [D, ntok], BF16)

    xs = P.sb("xs", [128, KC, TG], F32)
    mb = P.sb("mb", [128, KC, TG], BF16)
    hid = P.sb("hid", [128, HC, TG], BF16)
    wA = [P.sb(f"wA{i}", [128, KC, 512], BF16) for i in range(2)]
    wB = [P.sb(f"wB{i}", [128, 8, 512], BF16) for i in range(3)]
    S = {}
    S["sq"] = [P.sb(f"sq{i}", [128, TG], F32) for i in range(2)]
    S["tmp"] = [P.sb(f"tmp{i}", [128, TG], F32) for i in range(2)]
    S["mean"] = P.sb("mean", [128, TG], F32)
    S["rstd"] = P.sb("rstd", [128, TG], F32)
    S["nmr"] = P.sb("nmr", [128, TG], F32)
    S["onesN"] = P.sb("onesN", [128, 128], F32)
    lnp_sb = P.sb("lnp_sb", [128, 4 * KC], F32)
    S["pst"] = [P.ps(f"ps{i}", [128, 512]) for i in range(8)]
    S["psb"] = P.bufs(8, "ps")
    S["sq_b"] = P.bufs(2, "sq")
    S["tmp_b"] = P.bufs(2, "tmp")
    S["st_b"] = P.bufs(3, "st")
    S["ones_b"] = P.buf("ones")
    S["lnp_b"] = P.buf("lnp")
    xs_b = P.bufs(KC, "xs")
    mb_b = P.bufs(KC, "mb")
    hid_b = P.bufs(HC, "hid")
    wA_b = P.bufs(2, "wA")
    wB_b = P.bufs(3, "wB")
    oT_b = P.buf("oT")
    pst, psb = S["pst"], S["psb"]

    P.add("dve", lambda e: e.memset(S["onesN"][:, :], 1.0 / D), writes=[S["ones_b"]])
    P.dma("sp", lnp_sb[:, :], lnp, writes=[S["lnp_b"]])
    mTv = mT.rearrange("(kc p) t -> p kc t", p=128)
    xTv = xT.rearrange("(kc p) t -> p kc t", p=128)
    oTv = oT.rearrange("(kc p) t -> p kc t", p=128)
    oTbv = oTb.rearrange("(kc p) t -> p kc t", p=128)
    woutv = wout.rearrange("(kc p) n -> p kc n", p=128)
    w1v = w1.rearrange("(kc p) n -> p kc n", p=128)
    w2v = w2.rearrange("(hc p) n -> p hc n", p=128)
    wa_i = 0
    wb_i = 0
    bank = 0
    for tg in range(ntg):
        ts = slice(tg * TG, (tg + 1) * TG)
        P.dma("sp", mb[:, :, :], mTv[:, :, ts], writes=mb_b)
        P.dma("sp", xs[:, :, :], xTv[:, :, ts], writes=xs_b)
        for op in range(4):
            wi = wa_i % 2
            wa_i += 1
            P.dma("sp", wA[wi][:, :, :], woutv[:, :, op * 512:(op + 1) * 512], writes=[wA_b[wi]])
            for j in range(4):
                oc = op * 4 + j
                bk = bank % 4
                bank += 1
                for kc in range(KC):
                    P.mm(pst[bk][:, :], wA[wi][:, kc, j * 128:(j + 1) * 128], mb[:, kc, :], kc == 0, kc == KC - 1,
                         reads=[wA_b[wi], mb_b[kc]], writes=[psb[bk]])
                P.add("dve", lambda e, oc=oc, bk=bk: e.scalar_tensor_tensor(
                    out=xs[:, oc, :], in0=xs[:, oc, :], scalar=ALPHA, in1=pst[bk][:, :], op0=ALU.mult, op1=ALU.add),
                    reads=[xs_b[oc], psb[bk]], writes=[xs_b[oc]])
        layer_norm_fm(P, xs, xs_b, mb, mb_b, lnp_sb[:, 0:KC], lnp_sb[:, KC:2 * KC], S)
        for hp in range(HC // 4):
            wi = wa_i % 2
            wa_i += 1
            P.dma("sp", wA[wi][:, :, :], w1v[:, :, hp * 512:(hp + 1) * 512], writes=[wA_b[wi]])
            for j in range(4):
                hc = hp * 4 + j
                bk = bank % 4
                bank += 1
                for kc in range(KC):
                    P.mm(pst[bk][:, :], wA[wi][:, kc, j * 128:(j + 1) * 128], mb[:, kc, :], kc == 0, kc == KC - 1,
                         reads=[wA_b[wi], mb_b[kc]], writes=[psb[bk]])
                tj = hc % 2
                P.add("act", lambda e, bk=bk, tj=tj: e.activation(out=S["tmp"][tj][:, :], in_=pst[bk][:, :], func=AF.Relu),
                      reads=[psb[bk]], writes=[S["tmp_b"][tj]])
                P.add("pool", lambda e, hc=hc, tj=tj: e.tensor_tensor(out=hid[:, hc, :], in0=S["tmp"][tj][:, :],
                                                                      in1=S["tmp"][tj][:, :], op=ALU.mult),
                      reads=[S["tmp_b"][tj]], writes=[hid_b[hc]])
        for op in range(4):
            for hg in range(8):
                wi = wb_i % 3
                wb_i += 1
                P.dma("sp", wB[wi][:, :, :], w2v[:, hg * 8:(hg + 1) * 8, op * 512:(op + 1) * 512], writes=[wB_b[wi]])
                for h in range(8):
                    hc = hg * 8 + h
                    for j in range(4):
                        P.mm(pst[4 + j][:, :], wB[wi][:, h, j * 128:(j + 1) * 128], hid[:, hc, :],
                             hc == 0, hc == HC - 1, reads=[wB_b[wi], hid_b[hc]], writes=[psb[4 + j]])
            for j in range(4):
                oc = op * 4 + j
                P.add("dve", lambda e, oc=oc, j=j: e.scalar_tensor_tensor(
                    out=xs[:, oc, :], in0=xs[:, oc, :], scalar=ALPHA, in1=pst[4 + j][:, :], op0=ALU.mult, op1=ALU.add),
                    reads=[xs_b[oc], psb[4 + j]], writes=[xs_b[oc]])
        layer_norm_fm(P, xs, xs_b, mb, mb_b, lnp_sb[:, 2 * KC:3 * KC], lnp_sb[:, 3 * KC:4 * KC], S)
        P.dma("sp", oTv[:, :, ts], xs[:, :, :], reads=xs_b, writes=[oT_b])
        P.dma("sp", oTbv[:, :, ts], mb[:, :, :], reads=mb_b, writes=[oT_b])
    P.final_wait([oT_b])
    return P.emit(), P


NT = 1024
AW = 1024
NH = 8
WIN = 3072
CW = 31


def build_even_in():
    P = Prog()
    xT = P.din("xT", [D, NT], F32)
    win = P.din("win", [D, 5120], BF16)
    qT = P.dout("qT", [AW, NT], BF16)
    kT = P.dout("kT", [AW, NT], BF16)
    v = P.dout("v", [NT, AW], BF16)
    hT = P.dout("hT", [AW, NT], F32)
    xb = P.sb("xb", [128, KC, NT], BF16)
    wA = [P.sb(f"wA{i}", [128, KC, 512], BF16) for i in range(2)]
    st16 = [P.sb(f"st16_{i}", [128, 512], BF16) for i in range(3)]
    st32 = [P.sb(f"st32_{i}", [128, 512], F32) for i in range(2)]
    sg = [P.sb(f"sg{i}", [128, 512], F32) for i in range(2)]
    pst = [P.ps(f"ps{i}", [128, 512]) for i in range(6)]
    psb = P.bufs(6, "ps")
    xb_b = P.bufs(2, "xb")
    wA_b = P.bufs(2, "wA")
    st16_b = P.bufs(3, "st16")
    st32_b = P.bufs(2, "st32")
    sg_b = P.bufs(2, "sg")
    out_b = P.buf("out")
    xTv = xT.rearrange("(kc p) t -> p kc t", p=128)
    winv = win.rearrange("(kc p) n -> p kc n", p=128)
    for tg in range(2):
        P.dma("pool", xb[:, :, tg * 512:(tg + 1) * 512], xTv[:, :, tg * 512:(tg + 1) * 512], writes=[xb_b[tg]])
    cnt = dict(bank=0, s16=0, s32=0, w=0)

    def mm_fm(wi, j, tg, bk):
        for kc in range(KC):
            P.mm(pst[bk][:, :], wA[wi][:, kc, j * 128:(j + 1) * 128], xb[:, kc, tg * 512:(tg + 1) * 512],
                 kc == 0, kc == KC - 1, reads=[wA_b[wi], xb_b[tg]], writes=[psb[bk]])

    for pn in range(4):
        wi = cnt["w"] % 2
        cnt["w"] += 1
        P.dma("sp", wA[wi][:, :, :], winv[:, :, pn * 512:(pn + 1) * 512], writes=[wA_b[wi]])
        dst = qT if pn < 2 else kT
        scale = 128 ** -0.5 if pn < 2 else 1.0
        for j in range(4):
            oc = (pn % 2) * 4 + j
            for tg in range(2):
                bk = cnt["bank"] % 4
                cnt["bank"] += 1
                mm_fm(wi, j, tg, bk)
                si = cnt["s16"] % 3
                cnt["s16"] += 1
                P.add("act", lambda e, si=si, bk=bk, scale=scale: e.activation(out=st16[si][:, :], in_=pst[bk][:, :],
                                                                              func=AF.Copy, scale=scale),
                      reads=[psb[bk]], writes=[st16_b[si]])
                P.dma("sp", dst[oc * 128:(oc + 1) * 128, tg * 512:(tg + 1) * 512], st16[si][:, :],
                      reads=[st16_b[si]], writes=[out_b])
    for pn in range(2):
        wi = cnt["w"] % 2
        cnt["w"] += 1
        P.dma("sp", wA[wi][:, :, :], winv[:, :, 2048 + pn * 512:2048 + (pn + 1) * 512], writes=[wA_b[wi]])
        for tt in range(8):
            bk = cnt["bank"] % 4
            cnt["bank"] += 1
            for kc in range(KC):
                P.mm(pst[bk][:, :], xb[:, kc, tt * 128:(tt + 1) * 128], wA[wi][:, kc, :], kc == 0, kc == KC - 1,
                     reads=[wA_b[wi], xb_b[tt // 4]], writes=[psb[bk]])
            si = cnt["s16"] % 3
            cnt["s16"] += 1
            P.add("act", lambda e, si=si, bk=bk: e.activation(out=st16[si][:, :], in_=pst[bk][:, :], func=AF.Copy),
                  reads=[psb[bk]], writes=[st16_b[si]])
            P.dma("sp", v[tt * 128:(tt + 1) * 128, pn * 512:(pn + 1) * 512], st16[si][:, :],
                  reads=[st16_b[si]], writes=[out_b])
    for gp in range(4):
        wi = cnt["w"] % 2
        cnt["w"] += 1
        P.dma("sp", wA[wi][:, :, 0:256], winv[:, :, 3072 + gp * 256:3072 + (gp + 1) * 256], writes=[wA_b[wi]])
        P.dma("sp", wA[wi][:, :, 256:512], winv[:, :, 4096 + gp * 256:4096 + (gp + 1) * 256], writes=[wA_b[wi]])
        for j in range(2):
            cc = gp * 2 + j
            for tg in range(2):
                ba = 4
                bg = 5
                for kc in range(KC):
                    P.mm(pst[ba][:, :], wA[wi][:, kc, j * 128:(j + 1) * 128], xb[:, kc, tg * 512:(tg + 1) * 512],
                         kc == 0, kc == KC - 1, reads=[wA_b[wi], xb_b[tg]], writes=[psb[ba]])
                for kc in range(KC):
                    P.mm(pst[bg][:, :], wA[wi][:, kc, 256 + j * 128:256 + (j + 1) * 128], xb[:, kc, tg * 512:(tg + 1) * 512],
                         kc == 0, kc == KC - 1, reads=[wA_b[wi], xb_b[tg]], writes=[psb[bg]])
                si = cnt["s32"] % 2
                cnt["s32"] += 1
                P.add("act", lambda e, si=si: e.activation(out=sg[si][:, :], in_=pst[bg][:, :], func=AF.Sigmoid),
                      reads=[psb[bg]], writes=[sg_b[si]])
                P.add("dve", lambda e, si=si: e.tensor_tensor(out=st32[si][:, :], in0=pst[ba][:, :], in1=sg[si][:, :], op=ALU.mult),
                      reads=[psb[ba], sg_b[si]], writes=[st32_b[si]])
                P.dma("sp", hT[cc * 128:(cc + 1) * 128, tg * 512:(tg + 1) * 512], st32[si][:, :],
                      reads=[st32_b[si]], writes=[out_b])
    P.final_wait([out_b])
    return P.emit(), P


BRANCH = ((1, 8, 128, 256), (4, 2, 128, 256), (16, 1, 64, 192))
SLOPES = [2.0 ** (-(h + 1)) for h in range(NH)]
SW = 130


def build_even_mix():
    P = Prog()
    nc = P.nc
    qT = P.din("qT", [AW, NT], BF16)
    kTw = P.din("kTw", [AW, WIN], BF16)
    vd1 = P.din("vd1", [9, 128, AW], BF16)
    vd4 = P.din("vd4", [4, 3, 128, AW], BF16)
    vd16 = P.din("vd16", [16, 2, 128, AW], BF16)
    jm = P.din("jm", [128, 4, 256], F32)
    hTw = P.din("hTw", [AW, NT + CW - 1], F32)
    convw = P.din("convw", [128, 8, CW], F32)
    convp = P.din("convp", [128, 3, 8], F32)
    ident_in = P.din("ident", [128, 128], BF16)
    mT = P.dout("mT", [D, NT], BF16)
    OB = nc.dram_tensor("OB", [NT, 3, NH * SW], F32, kind="Internal").ap()

    q_sb = P.sb("q_sb", [128, NH, NT], BF16)
    k_sb = P.sb("k_sb", [128, NH, WIN], BF16)
    v_sb = [P.sb(f"v_sb{i}", [128, 2, AW], BF16) for i in range(2)]
    stage = [P.sb(f"stage{i}", [128, NH, SW], F32) for i in range(2)]
    s_sb = [P.sb(f"s_sb{i}", [128, 256], F32) for i in range(2)]
    p_bf = [P.sb(f"p_bf{i}", [128, 256], BF16) for i in range(2)]
    pt_sb = [P.sb(f"pt_sb{i}", [128, 2, 128], BF16) for i in range(2)]
    nm = [P.sb(f"nm{i}", [128, 1], F32) for i in range(2)]
    jm_sb = P.sb("jm_sb", [128, 4, 256], F32)
    ident = P.sb("ident_sb", [128, 128], BF16)
    hw = P.sb("hw", [128, 8, TG + CW - 1], F32)
    acc = P.sb("acc", [128, 8, TG], F32)
    cv_bf = P.sb("cv_bf", [128, 8, TG], BF16)
    cw_sb = P.sb("cw_sb", [128, 8, CW], F32)
    cp_sb = P.sb("cp_sb", [128, 3, 8], F32)
    cb = P.sb("cb", [128, 3, NH, SW], F32)
    cst = P.sb("cst", [128, 8, 3, NH], F32)
    att_bf = P.sb("att_bf", [128, NH, 128], BF16)
    attT = P.sb("attT", [128, NH, NT], BF16)
    S = {}
    S["sq"] = [P.sb(f"sq{i}", [128, TG], F32) for i in range(2)]
    S["tmp"] = [P.sb(f"tmp{i}", [128, TG], F32) for i in range(2)]
    S["mean"] = P.sb("mean", [128, TG], F32)
    S["rstd"] = P.sb("rstd", [128, TG], F32)
    S["nmr"] = P.sb("nmr", [128, TG], F32)
    onesC = P.sb("onesC", [128, 128], F32)
    S["onesN"] = onesC
    ps_A = [P.ps(f"ps_A{i}", [128, 512]) for i in range(2)]
    ps_T = [P.ps(f"ps_T{i}", [128, 4, 128], BF16) for i in range(2)]
    ps_s = [ps_A[i][:, 0:256] for i in range(2)]
    ps_o = [ps_A[i][:, 256:384] for i in range(2)]
    ps_t = [ps_T[i][:, 0:2, :] for i in range(2)]
    ps_tr = [ps_T[i][:, 2, :] for i in range(2)]
    ps_ln = [P.ps(f"ps_ln{i}", [128, 512]) for i in range(2)]
    S["pst"] = [None] * 4 + ps_ln
    S["psb"] = [None] * 4 + P.bufs(2, "psln")
    S["sq_b"] = P.bufs(2, "sq")
    S["tmp_b"] = P.bufs(2, "tmp")
    S["st_b"] = P.bufs(3, "st")
    S["ones_b"] = P.buf("ones")
    S["lnp_b"] = P.buf("cp")
    q_b, k_b, jm_b, id_b, cw_b = P.buf("q"), P.buf("k"), P.buf("jm"), P.buf("id"), P.buf("cw")
    v_b, stage_b, s_b, p_b, pt_b, nm_b = (P.bufs(2, n) for n in ("v", "stage", "s", "p", "pt", "nm"))
    pss_b, pst_b, pso_b, pstr_b = (P.bufs(2, n) for n in ("pss", "pst", "pso", "pstr"))
    hw_b, acc_b, cv_b = P.bufs(8, "hw"), P.bufs(8, "acc"), P.bufs(8, "cv")
    cb_b, cst_b, att_b, attT_b, OB_b, out_b = P.buf("cb"), P.buf("cst"), P.bufs(NH, "att"), P.bufs(NH, "attT"), P.buf("OB"), P.buf("out")

    P.dma("sp", q_sb[:, :, :], qT.rearrange("(h p) t -> p h t", p=128), writes=[q_b])
    P.dma("sp", k_sb[:, :, :], kTw.rearrange("(h p) t -> p h t", p=128), writes=[k_b])
    P.dma("sp", jm_sb[:, :, :], jm, writes=[jm_b])
    P.dma("sp", ident[:, :], ident_in, writes=[id_b])
    P.dma("sp", cw_sb[:, :, :], convw, writes=[cw_b])
    P.dma("sp", cp_sb[:, :, :], convp, writes=[S["lnp_b"]])
    P.add("dve", lambda e: e.memset(onesC[:, :], 1.0 / AW), writes=[S["ones_b"]])

    hTv = hTw.rearrange("(c p) t -> p c t", p=128)
    mTv = mT.rearrange("(c p) t -> p c t", p=128)
    for tg in range(2):
        P.dma("sp", hw[:, :, :], hTv[:, :, tg * TG:tg * TG + TG + CW - 1], writes=hw_b)
        for c in range(8):
            eng = "dve"
            P.add(eng, lambda e, c=c: e.tensor_scalar(out=acc[:, c, :], in0=hw[:, c, 0:TG], scalar1=cw_sb[:, c, 0:1],
                                                      scalar2=cp_sb[:, 0, c:c + 1], op0=ALU.mult, op1=ALU.add),
                  reads=[hw_b[c], cw_b, S["lnp_b"]], writes=[acc_b[c]])
            for j in range(1, CW):
                P.add(eng, lambda e, c=c, j=j: e.scalar_tensor_tensor(out=acc[:, c, :], in0=hw[:, c, j:j + TG],
                                                                      scalar=cw_sb[:, c, j:j + 1], in1=acc[:, c, :],
                                                                      op0=ALU.mult, op1=ALU.add),
                      reads=[hw_b[c], cw_b, acc_b[c]], writes=[acc_b[c]])

        def fin(kc, t_ap, t_buf):
            P.add("act", lambda e, kc=kc: e.activation(out=cv_bf[:, kc, :], in_=t_ap[:, :], func=AF.Silu,
                                                       scale=cp_sb[:, 1, kc:kc + 1], bias=cp_sb[:, 2, kc:kc + 1]),
                  reads=[t_buf, S["lnp_b"]], writes=[cv_b[kc]])
        layer_norm_fm(P, acc, acc_b, None, None, None, None, S, nk=8, final=fin)
        P.dma("sp", mTv[:, 8:16, tg * TG:(tg + 1) * TG], cv_bf[:, :, :], reads=cv_b, writes=[out_b])

    cnt = dict(blk=0, it=0)
    for bi, (d, nblk, nq, nk) in enumerate(BRANCH):
        for r in range(d):
            for qb in range(nblk):
                vi = cnt["blk"] % 2
                si = cnt["blk"] % 2
                cnt["blk"] += 1
                if d == 1:
                    P.dma("sp", v_sb[vi][:, :, :], vd1[qb:qb + 2].rearrange("j p f -> p j f"), writes=[v_b[vi]])
                elif d == 4:
                    P.dma("sp", v_sb[vi][:, :, :], vd4[r, qb:qb + 2].rearrange("j p f -> p j f"), writes=[v_b[vi]])
                else:
                    P.dma("sp", v_sb[vi][:, :, :], vd16[r].rearrange("j p f -> p j f"), writes=[v_b[vi]])
                var = (0 if qb == 0 else 1) if d < 16 else 2
                q0 = r + d * qb * 128
                iw0 = 2048 // d + qb * 128 - 128
                k0 = r + d * iw0
                kbs = [(0, 128), (128, nk - 128)]
                for h in range(NH):
                    i2 = cnt["it"] % 2
                    cnt["it"] += 1
                    a = SLOPES[h] * d
                    P.mm(ps_s[i2][0:nq, 0:nk], q_sb[:, h, q0:q0 + d * (nq - 1) + 1:d], k_sb[:, h, k0:k0 + d * (nk - 1) + 1:d], True, True,
                         reads=[q_b, k_b], writes=[pss_b[i2]])
                    P.add("dve", lambda e, i2=i2, a=a, var=var, nq=nq, nk=nk: e.scalar_tensor_tensor(
                        out=s_sb[i2][0:nq, 0:nk], in0=jm_sb[0:nq, var, 0:nk], scalar=a, in1=ps_s[i2][0:nq, 0:nk],
                        op0=ALU.mult, op1=ALU.add), reads=[jm_b, pss_b[i2]], writes=[s_b[i2]])
                    P.add("dve", lambda e, i2=i2, si=si, h=h, nq=nq, nk=nk: e.reduce_max(
                        out=stage[si][0:nq, h, 128:129], in_=s_sb[i2][0:nq, 0:nk], axis=AX.X),
                        reads=[s_b[i2]], writes=[stage_b[si]])
                    P.add("pool", lambda e, i2=i2, si=si, h=h, nq=nq: e.tensor_scalar(
                        out=nm[i2][0:nq, :], in0=stage[si][0:nq, h, 128:129], scalar1=-1.0, scalar2=None, op0=ALU.mult),
                        reads=[stage_b[si]], writes=[nm_b[i2]])
                    P.add("act", lambda e, i2=i2, si=si, h=h, nq=nq, nk=nk: e.activation(
                        out=p_bf[i2][0:nq, 0:nk], in_=s_sb[i2][0:nq, 0:nk], func=AF.Exp, bias=nm[i2][0:nq, :], scale=1.0,
                        accum_out=stage[si][0:nq, h, 129:130]),
                        reads=[s_b[i2], nm_b[i2]], writes=[p_b[i2], stage_b[si]])
                    for kb, (ks, kn) in enumerate(kbs):
                        P.add("pe", lambda e, i2=i2, kb=kb, ks=ks, kn=kn, nq=nq: e.transpose(
                            ps_t[i2][0:kn, kb, 0:nq], p_bf[i2][0:nq, ks:ks + kn], ident[0:nq, 0:nq]),
                            reads=[p_b[i2], id_b], writes=[pst_b[i2]])
                    for kb, (ks, kn) in enumerate(kbs):
                        P.add("act", lambda e, i2=i2, kb=kb, kn=kn, nq=nq: e.activation(
                            out=pt_sb[i2][0:kn, kb, 0:nq], in_=ps_t[i2][0:kn, kb, 0:nq], func=AF.Copy),
                            reads=[pst_b[i2]], writes=[pt_b[i2]])
                    for kb, (ks, kn) in enumerate(kbs):
                        P.mm(ps_o[i2][0:nq, :], pt_sb[i2][0:kn, kb, 0:nq], v_sb[vi][0:kn, kb, h * 128:(h + 1) * 128],
                             kb == 0, kb == 1, reads=[pt_b[i2], v_b[vi]], writes=[pso_b[i2]])
                    P.add("dve", lambda e, i2=i2, si=si, h=h, nq=nq: e.tensor_copy(
                        out=stage[si][0:nq, h, 0:128], in_=ps_o[i2][0:nq, :]),
                        reads=[pso_b[i2]], writes=[stage_b[si]])
                P.dma("sp", OB[q0:q0 + d * (nq - 1) + 1:d, bi, :], stage[si][0:nq, :, :].rearrange("p h w -> p (h w)"),
                      reads=[stage_b[si]], writes=[OB_b])

    mx = lambda b: cb[:, b, :, 128]
    for tt in range(NT // 128):
        P.dma("sp", cb[:, :, :, :].rearrange("p b h w -> p b (h w)"), OB[tt * 128:(tt + 1) * 128, :, :], reads=[OB_b], writes=[cb_b])
        M, cdf, cc, tden, den, coef = (cst[:, i, :, :] for i in range(6))
        P.add("dve", lambda e: e.tensor_tensor(out=M[:, 0, :], in0=mx(0), in1=mx(1), op=ALU.max), reads=[cb_b], writes=[cst_b])
        P.add("dve", lambda e: e.tensor_tensor(out=M[:, 0, :], in0=M[:, 0, :], in1=mx(2), op=ALU.max), reads=[cb_b, cst_b], writes=[cst_b])
        for b in range(3):
            P.add("dve", lambda e, b=b: e.tensor_tensor(out=cdf[:, b, :], in0=mx(b), in1=M[:, 0, :], op=ALU.subtract),
                  reads=[cb_b, cst_b], writes=[cst_b])
        P.add("act", lambda e: e.activation(out=cc[:, :, :], in_=cdf[:, :, :], func=AF.Exp), reads=[cst_b], writes=[cst_b])
        P.add("dve", lambda e: e.tensor_tensor(out=tden[:, :, :], in0=cc[:, :, :], in1=cb[:, :, :, 129], op=ALU.mult),
              reads=[cb_b, cst_b], writes=[cst_b])
        P.add("dve", lambda e: e.tensor_tensor(out=den[:, 0, :], in0=tden[:, 0, :], in1=tden[:, 1, :], op=ALU.add), reads=[cst_b], writes=[cst_b])
        P.add("dve", lambda e: e.tensor_tensor(out=den[:, 0, :], in0=den[:, 0, :], in1=tden[:, 2, :], op=ALU.add), reads=[cst_b], writes=[cst_b])
        P.add("dve", lambda e: e.reciprocal(out=den[:, 1, :], in_=den[:, 0, :]), reads=[cst_b], writes=[cst_b])
        for b in range(3):
            P.add("dve", lambda e, b=b: e.tensor_tensor(out=coef[:, b, :], in0=cc[:, b, :], in1=den[:, 1, :], op=ALU.mult),
                  reads=[cst_b], writes=[cst_b])
        for h in range(NH):
            eng = "dve" if h % 2 == 0 else "pool"
            P.add(eng, lambda e, h=h: e.tensor_scalar(out=cb[:, 0, h, 0:128], in0=cb[:, 0, h, 0:128], scalar1=coef[:, 0, h:h + 1],
                                                      scalar2=None, op0=ALU.mult), reads=[cb_b, cst_b], writes=[cb_b])
        for h in range(NH):
            eng = "dve"
            P.add(eng, lambda e, h=h: e.scalar_tensor_tensor(out=cb[:, 0, h, 0:128], in0=cb[:, 1, h, 0:128], scalar=coef[:, 1, h:h + 1],
                                                             in1=cb[:, 0, h, 0:128], op0=ALU.mult, op1=ALU.add),
                  reads=[cb_b, cst_b], writes=[cb_b])
        for h in range(NH):
            eng = "dve"
            P.add(eng, lambda e, h=h: e.scalar_tensor_tensor(out=att_bf[:, h, :], in0=cb[:, 2, h, 0:128], scalar=coef[:, 2, h:h + 1],
                                                             in1=cb[:, 0, h, 0:128], op0=ALU.mult, op1=ALU.add),
                  reads=[cb_b, cst_b], writes=[att_b[h]])
        for h in range(NH):
            i2 = h % 2
            P.add("pe", lambda e, h=h, i2=i2: e.transpose(ps_tr[i2][:, :], att_bf[:, h, :], ident[:, :]),
                  reads=[att_b[h], id_b], writes=[pstr_b[i2]])
            P.add("act", lambda e, h=h, i2=i2, tt=tt: e.activation(out=attT[:, h, tt * 128:(tt + 1) * 128], in_=ps_tr[i2][:, :], func=AF.Copy),
                  reads=[pstr_b[i2]], writes=[attT_b[h]])
    P.dma("sp", mTv[:, 0:8, :], attT[:, :, :], reads=attT_b, writes=[out_b])
    P.final_wait([out_b])
    return P.emit(), P


def make_jm(first):
    jm = np.full((128, 4, 256), -1e9, np.float32)
    q = np.arange(128)[:, None]
    kk = np.arange(256)[None, :]
    j = 128 + q - kk
    valid = (j >= 0) & (j <= 128)
    base = np.where(valid, -j.astype(np.float32), -1e9).astype(np.float32)
    jm[:, 1, :] = base
    jm[:, 0, :] = np.where(kk < 128, -1e9, base) if first else base
    kk2 = np.arange(256)[None, :]
    j2 = 128 + q - kk2
    valid2 = (j2 >= 0) & (j2 <= 128) & (kk2 < 192)
    b2 = np.where(valid2, -j2.astype(np.float32), -1e9).astype(np.float32)
    jm[:, 2, :] = np.where(kk2 < 128, -1e9, b2) if first else b2
    jm[:, 3, :] = b2
    return jm


def even_mix_inputs(c, kT_all, v_all, hT_all, qT_c, conv_w, conv_b, ln_g, ln_b, ident):
    b, qtr = divmod(c, 4)
    dt = kT_all[0].dtype
    kTw = np.zeros((AW, WIN), dt)
    vw = np.zeros((WIN, AW), v_all[0].dtype)
    for w_q in range(3):
        src = qtr - 2 + w_q
        if src >= 0:
            kTw[:, w_q * NT:(w_q + 1) * NT] = kT_all[b * 4 + src]
            vw[w_q * NT:(w_q + 1) * NT] = v_all[b * 4 + src]
    vd1 = vw[15 * 128:24 * 128].reshape(9, 128, AW)
    vd4 = np.zeros((4, 3, 128, AW), vw.dtype)
    for r in range(4):
        sub = vw[r::4]
        vd4[r] = sub[3 * 128:6 * 128].reshape(3, 128, AW)
    vd16 = np.zeros((16, 2, 128, AW), vw.dtype)
    for r in range(16):
        sub = vw[r::16]
        vd16[r, 0] = sub[0:128]
        vd16[r, 1, 0:64] = sub[128:192]
    hTw = np.zeros((AW, NT + CW - 1), hT_all[0].dtype)
    hTw[:, CW - 1:] = hT_all[c]
    if qtr > 0:
        hTw[:, :CW - 1] = hT_all[c - 1][:, NT - (CW - 1):]
    convw = np.ascontiguousarray(conv_w.reshape(CW, 8, 128).transpose(2, 1, 0))
    convp = np.ascontiguousarray(np.stack([conv_b, ln_g, ln_b]).reshape(3, 8, 128).transpose(2, 0, 1))
    return dict(qT=qT_c, kTw=kTw, vd1=np.ascontiguousarray(vd1), vd4=vd4, vd16=vd16, jm=make_jm(qtr == 0),
                hTw=hTw, convw=convw, convp=convp, ident=ident)


def odd_mix_inputs(c, xT_full_b, w_in_bf, igb, fgb, norm_g, consts):
    hd = c % 4
    cols = np.concatenate([np.arange(hd * 256, hd * 256 + 256), 1024 + np.arange(hd * 256, hd * 256 + 256),
                           [6144 + hd, 6148 + hd], 2048 + np.arange(hd * 512, hd * 512 + 512),
                           4096 + np.arange(hd * 512, hd * 512 + 512)])
    wh = np.ascontiguousarray(w_in_bf[:, cols])
    gb = np.ascontiguousarray(np.broadcast_to(np.array([igb[hd], fgb[hd]], np.float32), (128, 2)))
    ng = np.ascontiguousarray(np.broadcast_to(norm_g[hd * 512:(hd + 1) * 512].astype(np.float32), (128, 512)))
    return dict(xT=xT_full_b, wh=wh, gb=gb, ng=ng, cst=consts)


SEQ = 4096
NCH = 32
DK = 256
DV = 512
WC = 2 * DK + 2 + 2 * DV


def build_odd_mix():
    P = Prog()
    nc = P.nc
    xT = P.din("xT", [D, SEQ], F32)
    wh = P.din("wh", [D, WC], BF16)
    gb = P.din("gb", [128, 2], F32)
    ng = P.din("ng", [128, DV], F32)
    cst_in = P.din("cst", [128, 4, 128], F32)
    y = P.dout("y", [SEQ, DV], BF16)

    W = P.sb("W", [128, KC, WC], BF16)
    xb = P.sb("xb", [128, KC, TG], BF16)
    qT = P.sb("qT", [128, 2, SEQ], BF16)
    kT = P.sb("kT", [128, 2, SEQ], BF16)
    k_tm = P.sb("k_tm", [128, NCH, DK], BF16)
    v_tm = P.sb("v_tm", [128, NCH, DV], BF16)
    o_tm = P.sb("o_tm", [128, NCH, DV], BF16)
    g_tm = P.sb("g_tm", [128, NCH, 2], F32)
    C = P.sb("C", [128, 2, DV], F32)
    Cbf = P.sb("Cbf", [128, 2, DV], BF16)
    nst = P.sb("nst", [128, 2], F32)
    nbf = P.sb("nbf", [128, 2], BF16)
    onesb = P.sb("onesb", [128, 1], BF16)
    kw = [P.sb(f"kw{i}", [128, DK], BF16) for i in range(2)]
    sqk = [P.sb(f"sqk{i}", [128, 128], BF16) for i in range(2)]
    qa = [P.sb(f"qa{i}", [128, 2, 128], BF16) for i in range(2)]
    hsb = P.sb("hsb", [128, DV], F32)
    junk = P.sb("junk", [128, DV], F32)
    ysb = [P.sb(f"ysb{i}", [128, DV], BF16) for i in range(2)]
    ng_sb = P.sb("ng_sb", [128, DV], F32)
    gb_sb = P.sb("gb_sb", [128, 2], F32)
    cst = P.sb("cst_sb", [128, 4, 128], F32)
    mask01, tri, ones, ident = (cst[:, i, :] for i in range(4))
    G = P.sb("G", [128, 16, NCH], F32)
    ub = P.sb("ub", [32, 128], F32)
    small = [P.sb(f"small{i}", [128, 8], F32) for i in range(2)]
    pst = [P.ps(f"ps{i}", [128, 512]) for i in range(8)]
    psb = P.bufs(8, "ps")
    W_b, xb_b, qT_b, kT_b = P.buf("W"), P.buf("xb"), P.bufs(8, "qT"), P.bufs(8, "kT")
    ktm_b, vtm_b, otm_b, gtm_b = P.bufs(NCH, "ktm"), P.bufs(NCH, "vtm"), P.bufs(NCH, "otm"), P.buf("gtm")
    C_b, Cbf_b, n_b, nbf_b = P.bufs(2, "C"), P.bufs(2, "Cbf"), P.buf("n"), P.buf("nbf")
    kw_b, sqk_b, qa_b, ysb_b, small_b = (P.bufs(2, n) for n in ("kw", "sqk", "qa", "ysb", "small"))
    hsb_b, junk_b, ng_b, gb_b, cst_b, G_b, ub_b, onesb_b, y_b = (P.buf(n) for n in "hsb junk ng gb cst G ub onesb y".split())

    P.dma("sp", W[:, :, :], wh.rearrange("(kc p) n -> p kc n", p=128), writes=[W_b])
    P.dma("sp", ng_sb[:, :], ng, writes=[ng_b])
    P.dma("sp", gb_sb[:, :], gb, writes=[gb_b])
    P.dma("sp", cst[:, :, :], cst_in, writes=[cst_b])
    P.add("dve", lambda e: e.memset(C[:, :, :], 0.0), writes=C_b)
    P.add("dve", lambda e: e.memset(Cbf[:, :, :], 0.0), writes=Cbf_b)
    P.add("dve", lambda e: e.memset(nst[:, :], 0.0), writes=[n_b])
    P.add("dve", lambda e: e.memset(nbf[:, :], 0.0), writes=[nbf_b])
    P.add("dve", lambda e: e.memset(onesb[:, :], 1.0), writes=[onesb_b])

    xTv = xT.rearrange("(kc p) t -> p kc t", p=128)
    bank = 0
    QS = DK ** -0.5
    for tg in range(SEQ // TG):
        ts = slice(tg * TG, (tg + 1) * TG)
        P.dma("pool", xb[:, :, :], xTv[:, :, ts], writes=[xb_b])
        for which, dst, dst_b, scale in ((0, qT, qT_b, QS), (1, kT, kT_b, 1.0)):
            for dc in range(2):
                bk = bank % 4
                bank += 1
                col = which * DK + dc * 128
                for kc in range(KC):
                    P.mm(pst[bk][:, :], W[:, kc, col:col + 128], xb[:, kc, :], kc == 0, kc == KC - 1,
                         reads=[W_b, xb_b], writes=[psb[bk]])
                P.add("act", lambda e, dst=dst, dc=dc, ts=ts, bk=bk, scale=scale: e.activation(
                    out=dst[:, dc, ts], in_=pst[bk][:, :], func=AF.Copy, scale=scale),
                    reads=[psb[bk]], writes=[dst_b[tg]])
        for tt in range(4):
            ch = tg * 4 + tt
            tok = slice(tt * 128, (tt + 1) * 128)
            bk = bank % 4
            bank += 1
            for kc in range(KC):
                P.mm(pst[bk][:, 0:DK + 2], xb[:, kc, tok], W[:, kc, DK:2 * DK + 2], kc == 0, kc == KC - 1,
                     reads=[W_b, xb_b], writes=[psb[bk]])
            P.add("act", lambda e, ch=ch, bk=bk: e.activation(out=k_tm[:, ch, :], in_=pst[bk][:, 0:DK], func=AF.Copy),
                  reads=[psb[bk]], writes=[ktm_b[ch]])
            P.add("dve", lambda e, ch=ch, bk=bk: e.tensor_tensor(out=g_tm[:, ch, :], in0=pst[bk][:, DK:DK + 2], in1=gb_sb[:, :], op=ALU.add),
                  reads=[psb[bk], gb_b], writes=[gtm_b])
            bk = bank % 4
            bank += 1
            c0 = 2 * DK + 2
            for kc in range(KC):
                P.mm(pst[bk][:, :], xb[:, kc, tok], W[:, kc, c0:c0 + DV], kc == 0, kc == KC - 1,
                     reads=[W_b, xb_b], writes=[psb[bk]])
            P.add("dve", lambda e, ch=ch, bk=bk: e.tensor_copy(out=v_tm[:, ch, :], in_=pst[bk][:, :]),
                  reads=[psb[bk]], writes=[vtm_b[ch]])
            bk = bank % 4
            bank += 1
            c0 = 2 * DK + 2 + DV
            for kc in range(KC):
                P.mm(pst[bk][:, :], xb[:, kc, tok], W[:, kc, c0:c0 + DV], kc == 0, kc == KC - 1,
                     reads=[W_b, xb_b], writes=[psb[bk]])
            P.add("act", lambda e, ch=ch, bk=bk: e.activation(out=o_tm[:, ch, :], in_=pst[bk][:, :], func=AF.Sigmoid),
                  reads=[psb[bk]], writes=[otm_b[ch]])

    IG, LF, Bc, U, BL, UM, MP, M, Wt, A, E, T0, T1 = (G[:, i, :] for i in range(13))
    gr = [G_b]
    P.add("dve", lambda e: e.tensor_copy(out=IG, in_=g_tm[:, :, 0]), reads=[gtm_b], writes=gr)
    P.add("act", lambda e: e.activation(out=T0, in_=g_tm[:, :, 1], func=AF.Exp, scale=-1.0), reads=[gtm_b], writes=gr)
    P.add("dve", lambda e: e.tensor_scalar(out=T0, in0=T0, scalar1=1.0, scalar2=None, op0=ALU.add), reads=gr, writes=gr)
    P.add("act", lambda e: e.activation(out=T1, in_=T0, func=AF.Ln), reads=gr, writes=gr)
    P.add("dve", lambda e: e.tensor_scalar(out=LF, in0=T1, scalar1=-1.0, scalar2=None, op0=ALU.mult), reads=gr, writes=gr)
    P.mm(pst[4][:, 0:NCH], tri, LF, True, True, reads=[cst_b, G_b], writes=[psb[4]])
    P.mm(pst[5][:, 0:NCH], ones, LF, True, True, reads=[cst_b, G_b], writes=[psb[5]])
    P.add("dve", lambda e: e.tensor_copy(out=Bc, in_=pst[4][:, 0:NCH]), reads=[psb[4]], writes=gr)
    P.add("dve", lambda e: e.tensor_copy(out=BL, in_=pst[5][:, 0:NCH]), reads=[psb[5]], writes=gr)
    P.add("dve", lambda e: e.tensor_tensor(out=U, in0=IG, in1=Bc, op=ALU.subtract), reads=gr, writes=gr)
    P.add("pe", lambda e: e.transpose(pst[6][0:NCH, 0:128], U, ident), reads=[G_b, cst_b], writes=[psb[6]])
    P.add("dve", lambda e: e.reduce_max(out=small[0][0:NCH, 0:1], in_=pst[6][0:NCH, 0:128], axis=AX.X), reads=[psb[6]], writes=[small_b[0]])
    P.add("dve", lambda e: e.tensor_scalar(out=ub[:, :], in0=ones[0:NCH, :], scalar1=small[0][0:NCH, 0:1], scalar2=None, op0=ALU.mult),
          reads=[small_b[0], cst_b], writes=[ub_b])
    P.mm(pst[7][:, 0:NCH], ub[:, :], ident[0:NCH, 0:NCH], True, True, reads=[ub_b, cst_b], writes=[psb[7]])
    P.add("dve", lambda e: e.tensor_copy(out=UM, in_=pst[7][:, 0:NCH]), reads=[psb[7]], writes=gr)
    P.add("dve", lambda e: e.memset(MP[:, 0:1], 0.0), reads=gr, writes=gr)
    for c in range(NCH):
        P.add("dve", lambda e, c=c: e.tensor_tensor(out=M[:, c:c + 1], in0=MP[:, c:c + 1], in1=UM[:, c:c + 1], op=ALU.max), reads=gr, writes=gr)
        if c + 1 < NCH:
            P.add("dve", lambda e, c=c: e.tensor_tensor(out=MP[:, c + 1:c + 2], in0=BL[:, c:c + 1], in1=M[:, c:c + 1], op=ALU.add), reads=gr, writes=gr)
    P.add("dve", lambda e: e.tensor_tensor(out=T0, in0=U, in1=M, op=ALU.subtract), reads=gr, writes=gr)
    P.add("act", lambda e: e.activation(out=Wt, in_=T0, func=AF.Exp), reads=gr, writes=gr)
    P.add("dve", lambda e: e.tensor_tensor(out=T1, in0=MP, in1=M, op=ALU.subtract), reads=gr, writes=gr)
    P.add("act", lambda e: e.activation(out=A, in_=T1, func=AF.Exp), reads=gr, writes=gr)
    P.add("dve", lambda e: e.tensor_tensor(out=T0, in0=Bc, in1=M, op=ALU.add), reads=gr, writes=gr)
    P.add("act", lambda e: e.activation(out=E, in_=T0, func=AF.Exp, scale=-1.0), reads=gr, writes=gr)

    for c in range(NCH):
        i2 = c % 2
        cs = slice(c * 128, (c + 1) * 128)
        tgi = c // 4
        pA = pst[i2]
        pN = pst[2 + i2]
        pC = [pst[4 + 2 * i2], pst[5 + 2 * i2]]
        bA, bN, bC = psb[i2], psb[2 + i2], [psb[4 + 2 * i2], psb[5 + 2 * i2]]
        P.add("pool", lambda e, c=c, i2=i2: e.tensor_scalar(out=kw[i2][:, :], in0=k_tm[:, c, :], scalar1=Wt[:, c:c + 1], scalar2=None, op0=ALU.mult),
              reads=[ktm_b[c], G_b], writes=[kw_b[i2]])
        P.add("pool", lambda e, c=c, i2=i2, cs=cs: e.tensor_scalar(out=qa[i2][:, :, :], in0=qT[:, :, cs], scalar1=A[:, c:c + 1], scalar2=None, op0=ALU.mult),
              reads=[qT_b[tgi], G_b], writes=[qa_b[i2]])
        for dc in range(2):
            P.mm(pA[:, 0:128], kT[:, dc, cs], qT[:, dc, cs], dc == 0, dc == 1, reads=[kT_b[tgi], qT_b[tgi]], writes=[bA])
        P.add("dve", lambda e, c=c, i2=i2, pA=pA: e.scalar_tensor_tensor(out=sqk[i2][:, :], in0=pA[:, 0:128], scalar=Wt[:, c:c + 1], in1=mask01,
                                                                         op0=ALU.mult, op1=ALU.mult),
              reads=[bA, G_b, cst_b], writes=[sqk_b[i2]])
        for dc in range(2):
            P.mm(pN[:, :], qa[i2][:, dc, :], Cbf[:, dc, :], dc == 0, False, reads=[qa_b[i2], Cbf_b[dc]], writes=[bN])
        P.mm(pN[:, :], sqk[i2][:, :], v_tm[:, c, :], False, True, reads=[sqk_b[i2], vtm_b[c]], writes=[bN])
        for dc in range(2):
            P.mm(pA[:, 128:129], qa[i2][:, dc, :], nbf[:, dc:dc + 1], dc == 0, False, reads=[qa_b[i2], nbf_b], writes=[bA])
        P.mm(pA[:, 128:129], sqk[i2][:, :], onesb[:, :], False, True, reads=[sqk_b[i2], onesb_b], writes=[bA])
        for dc in range(2):
            P.mm(pC[dc][:, :], kw[i2][:, dc * 128:(dc + 1) * 128], v_tm[:, c, :], True, True, reads=[kw_b[i2], vtm_b[c]], writes=[bC[dc]])
        for dc in range(2):
            P.mm(pA[:, 130 + dc:131 + dc], kw[i2][:, dc * 128:(dc + 1) * 128], onesb[:, :], True, True, reads=[kw_b[i2], onesb_b], writes=[bA])
        sm = small[i2]
        P.add("dve", lambda e, sm=sm, pA=pA: e.tensor_scalar(out=sm[:, 0:1], in0=pA[:, 128:129], scalar1=-1.0, scalar2=None, op0=ALU.mult),
              reads=[bA], writes=[small_b[i2]])
        P.add("dve", lambda e, sm=sm, pA=pA: e.tensor_tensor(out=sm[:, 0:1], in0=sm[:, 0:1], in1=pA[:, 128:129], op=ALU.max),
              reads=[bA, small_b[i2]], writes=[small_b[i2]])
        P.add("dve", lambda e, c=c, sm=sm: e.tensor_tensor(out=sm[:, 0:1], in0=sm[:, 0:1], in1=E[:, c:c + 1], op=ALU.max),
              reads=[small_b[i2], G_b], writes=[small_b[i2]])
        P.add("dve", lambda e, sm=sm: e.reciprocal(out=sm[:, 1:2], in_=sm[:, 0:1]), reads=[small_b[i2]], writes=[small_b[i2]])
        P.add("act", lambda e, sm=sm, pN=pN: e.activation(out=hsb[:, :], in_=pN[:, :], func=AF.Copy, scale=sm[:, 1:2], accum_out=sm[:, 2:3]),
              reads=[bN, small_b[i2]], writes=[hsb_b, small_b[i2]])
        P.add("act", lambda e, sm=sm: e.activation(out=junk[:, :], in_=hsb[:, :], func=AF.Square, accum_out=sm[:, 3:4]),
              reads=[hsb_b], writes=[junk_b, small_b[i2]])
        P.add("dve", lambda e, sm=sm: e.tensor_scalar(out=sm[:, 4:5], in0=sm[:, 2:3], scalar1=1.0 / DV, scalar2=None, op0=ALU.mult),
              reads=[small_b[i2]], writes=[small_b[i2]])
        P.add("dve", lambda e, sm=sm: e.tensor_tensor(out=sm[:, 5:6], in0=sm[:, 4:5], in1=sm[:, 4:5], op=ALU.mult),
              reads=[small_b[i2]], writes=[small_b[i2]])
        P.add("dve", lambda e, sm=sm: e.scalar_tensor_tensor(out=sm[:, 6:7], in0=sm[:, 3:4], scalar=1.0 / DV, in1=sm[:, 5:6], op0=ALU.mult, op1=ALU.subtract),
              reads=[small_b[i2]], writes=[small_b[i2]])
        P.add("dve", lambda e, sm=sm: e.tensor_scalar(out=sm[:, 6:7], in0=sm[:, 6:7], scalar1=EPS, scalar2=None, op0=ALU.add),
              reads=[small_b[i2]], writes=[small_b[i2]])
        P.add("act", lambda e, sm=sm: e.activation(out=sm[:, 6:7], in_=sm[:, 6:7], func=AF.Sqrt), reads=[small_b[i2]], writes=[small_b[i2]])
        P.add("dve", lambda e, sm=sm: e.reciprocal(out=sm[:, 7:8], in_=sm[:, 6:7]), reads=[small_b[i2]], writes=[small_b[i2]])
        P.add("dve", lambda e, sm=sm: e.tensor_scalar(out=hsb[:, :], in0=hsb[:, :], scalar1=sm[:, 4:5], scalar2=sm[:, 7:8], op0=ALU.subtract, op1=ALU.mult),
              reads=[hsb_b, small_b[i2]], writes=[hsb_b])
        P.add("pool", lambda e: e.tensor_tensor(out=hsb[:, :], in0=hsb[:, :], in1=ng_sb[:, :], op=ALU.mult), reads=[hsb_b, ng_b], writes=[hsb_b])
        P.add("pool", lambda e, c=c, i2=i2: e.tensor_tensor(out=ysb[i2][:, :], in0=hsb[:, :], in1=o_tm[:, c, :], op=ALU.mult),
              reads=[hsb_b, otm_b[c]], writes=[ysb_b[i2]])
        P.dma("sp", y[cs, :], ysb[i2][:, :], reads=[ysb_b[i2]], writes=[y_b])
        for dc in range(2):
            P.add("dve", lambda e, c=c, dc=dc, pC=pC: e.scalar_tensor_tensor(out=C[:, dc, :], in0=C[:, dc, :], scalar=A[:, c:c + 1], in1=pC[dc][:, :],
                                                                              op0=ALU.mult, op1=ALU.add),
                  reads=[C_b[dc], G_b, bC[dc]], writes=[C_b[dc]])
            P.add("act", lambda e, dc=dc: e.activation(out=Cbf[:, dc, :], in_=C[:, dc, :], func=AF.Copy), reads=[C_b[dc]], writes=[Cbf_b[dc]])
        P.add("dve", lambda e, c=c, pA=pA: e.scalar_tensor_tensor(out=nst[:, :], in0=nst[:, :], scalar=A[:, c:c + 1], in1=pA[:, 130:132],
                                                                  op0=ALU.mult, op1=ALU.add),
              reads=[n_b, G_b, bA], writes=[n_b])
        P.add("dve", lambda e: e.tensor_copy(out=nbf[:, :], in_=nst[:, :]), reads=[n_b], writes=[nbf_b])
    P.final_wait([y_b])
    return P.emit(), P


def odd_consts():
    c = np.zeros((128, 4, 128), np.float32)
    i = np.arange(128)
    c[:, 0, :] = (i[:, None] <= i[None, :])
    c[:, 1, :] = (i[:, None] <= i[None, :])
    c[:, 2, :] = 1.0
    c[:, 3, :] = np.eye(128)
    return c


WROWS = 12034


def build_wconv():
    P = Prog()
    w = P.din("w", [WROWS, 2048], F32)
    o = P.dout("o", [WROWS, 2048], BF16)
    ob = P.buf("o")
    step = 1024
    r = 0
    while r < WROWS:
        n = min(step, WROWS - r)
        P.dma("pool", o[r:r + n, :], w[r:r + n, :], writes=[ob])
        r += n
    P.final_wait([ob])
    return P.emit(), P


_CACHE = {}


def _prog(name, fn):
    if name not in _CACHE:
        _CACHE[name] = fn()[0]
    return _CACHE[name]


def _run(name, fn, in_maps):
    nc = fn()[0]
    res = run_bass_kernel_spmd(nc, in_maps, core_ids=list(range(8)))
    return res.results


def kernel(x, even_w_in, even_conv_w, even_conv_b, even_conv_ln_g, even_conv_ln_b, even_w_out,
           odd_w_in, odd_igate_b, odd_fgate_b, odd_norm_g, odd_w_out, mix_ln_g, mix_ln_b,
           ffn_w1, ffn_w2, ffn_ln_g, ffn_ln_b):
    f32 = lambda a: np.ascontiguousarray(np.asarray(a, dtype=np.float32))
    x = f32(x)
    ws = [f32(a) for a in (even_w_in, even_w_out, odd_w_in, odd_w_out, ffn_w1, ffn_w2)]
    shapes = [a.shape for a in ws]
    flat = np.concatenate([a.reshape(-1, 2048) for a in ws], axis=0)
    assert flat.shape[0] == 8 * WROWS
    r = _run("wconv", build_wconv, [dict(w=flat[c * WROWS:(c + 1) * WROWS]) for c in range(8)])
    flat_bf = np.concatenate([r[c]["o"] for c in range(8)], axis=0)
    wbf = []
    off = 0
    for sh in shapes:
        n = int(np.prod(sh)) // 2048
        wbf.append(flat_bf[off:off + n].reshape(sh))
        off += n
    e_win, e_wout, o_win, o_wout, w1, w2 = wbf
    ident = np.eye(128, dtype=np.float32).astype(NPBF)
    oconst = odd_consts()
    mg, mb_, fg, fb = f32(mix_ln_g), f32(mix_ln_b), f32(ffn_ln_g), f32(ffn_ln_b)
    xT = [np.ascontiguousarray(x[c // 4, (c % 4) * 1024:(c % 4 + 1) * 1024].T) for c in range(8)]
    for layer in range(4):
        j = layer // 2
        if layer % 2 == 0:
            r1 = _run("even_in", build_even_in, [dict(xT=xT[c], win=e_win[j]) for c in range(8)])
            kT_all = [q["kT"] for q in r1]
            v_all = [q["v"] for q in r1]
            hT_all = [q["hT"] for q in r1]
            ims = [even_mix_inputs(c, kT_all, v_all, hT_all, r1[c]["qT"], f32(even_conv_w)[j], f32(even_conv_b)[j],
                                   f32(even_conv_ln_g)[j], f32(even_conv_ln_b)[j], ident) for c in range(8)]
            r2 = _run("even_mix", build_even_mix, ims)
            mT = [q["mT"] for q in r2]
            wout = e_wout[j]
        else:
            xfull = [np.ascontiguousarray(np.concatenate([xT[b * 4 + q] for q in range(4)], axis=1)) for b in range(2)]
            ims = [odd_mix_inputs(c, xfull[c // 4], o_win[j], f32(odd_igate_b)[j], f32(odd_fgate_b)[j], f32(odd_norm_g)[j], oconst)
                   for c in range(8)]
            r2 = _run("odd_mix", build_odd_mix, ims)
            mT = []
            for c in range(8):
                b, qtr = divmod(c, 4)
                mT.append(np.ascontiguousarray(np.concatenate(
                    [r2[b * 4 + hd]["y"][qtr * 1024:(qtr + 1) * 1024, :].T for hd in range(4)], axis=0)))
            wout = o_wout[j]
        lnp = np.ascontiguousarray(np.stack([mg[layer], mb_[layer], fg[layer], fb[layer]]).reshape(4, KC, 128)
                                   .transpose(2, 0, 1).reshape(128, 4 * KC))
        r3 = _run("tail", build_tail, [dict(mT=mT[c], xT=xT[c], wout=wout, w1=w1[layer], w2=w2[layer], lnp=lnp) for c in range(8)])
        xT = [q["oT"] for q in r3]
    out = np.empty((2, 4096, 2048), np.float32)
    for c in range(8):
        out[c // 4, (c % 4) * 1024:(c % 4 + 1) * 1024] = xT[c].T
    return out
```
